# Optimizing a Trainium2 kernel written in Bass

```python
import numpy as np
import jax, jax.numpy as jnp
from jax import lax

D_MODEL = 1024
BATCH = 8
SEQ = 4096
DEPTH = 1

HEAD_DIM = 64
ROT_DIM = HEAD_DIM // 4
ROPE_THETA = 500000.0
NORM_EPS = 1e-6

NSA_HEADS = 8
NSA_KV_GROUPS = 2
NSA_GROUP_SIZE = NSA_HEADS // NSA_KV_GROUPS
CMP_BLOCK = 32
CMP_STRIDE = 16
SEL_BLOCK = 64
SEL_TOPN = 16
WINDOW = 512
NSA_Q_CHUNK = 32

MOBA_HEADS = 8
MOBA_BLOCK = 256
MOBA_TOPK = 3
MOBA_Q_CHUNK = 16

D_FF = 4 * D_MODEL

NSA_WIDTH = NSA_HEADS * HEAD_DIM
KV_WIDTH = NSA_KV_GROUPS * HEAD_DIM
MOBA_WIDTH = MOBA_HEADS * HEAD_DIM
OFF_KV = NSA_WIDTH
OFF_GN = OFF_KV + 6 * KV_WIDTH
OFF_M = OFF_GN + 3 * NSA_HEADS
OFF_GM = OFF_M + 3 * MOBA_WIDTH
IN_WIDTH = OFF_GM + 2 * D_MODEL

kernel_name = "hybrid_nsa_moba_gated_block"


def rmsnorm(x, g):
    xf = x.astype(jnp.float32)
    y = xf * lax.rsqrt(jnp.mean(xf * xf, axis=-1, keepdims=True) + NORM_EPS)
    return (y * g.astype(jnp.float32)).astype(x.dtype)


def rope_tables(seq):
    inv = ROPE_THETA ** (-jnp.arange(0, ROT_DIM, 2, dtype=jnp.float32) / ROT_DIM)
    ang = jnp.arange(seq, dtype=jnp.float32)[:, None] * inv[None, :]
    return jnp.cos(ang), jnp.sin(ang)


def partial_rope(x, cos, sin):
    half = ROT_DIM // 2
    x1 = x[..., :half].astype(jnp.float32)
    x2 = x[..., half:ROT_DIM].astype(jnp.float32)
    c = cos[:, None, :]
    s = sin[:, None, :]
    rot = jnp.concatenate([x1 * c - x2 * s, x1 * s + x2 * c], axis=-1).astype(x.dtype)
    return jnp.concatenate([rot, x[..., ROT_DIM:]], axis=-1)


def masked_softmax(s, mask):
    s = jnp.where(mask, s.astype(jnp.float32), -jnp.inf)
    m = jnp.max(s, axis=-1, keepdims=True)
    m = jnp.where(jnp.isfinite(m), m, 0.0)
    p = jnp.where(mask, jnp.exp(s - m), 0.0)
    return p / jnp.maximum(jnp.sum(p, axis=-1, keepdims=True), 1e-30)


def compress_blocks(k, pe, w1, w2):
    b, seq, g, d = k.shape
    n_cmp = (seq - CMP_BLOCK) // CMP_STRIDE + 1
    idx = jnp.arange(n_cmp)[:, None] * CMP_STRIDE + jnp.arange(CMP_BLOCK)[None, :]
    kb = k[:, idx] + pe[None, None, :, None, :]
    kb = jnp.transpose(kb, (0, 3, 1, 2, 4)).reshape(b, g, n_cmp, CMP_BLOCK * d)
    return jax.nn.gelu(kb @ w1) @ w2


def cmp_to_sel_overlap(n_cmp, n_blk):
    i = np.arange(n_cmp)[:, None]
    j = np.arange(n_blk)[None, :]
    start = i * CMP_STRIDE
    end = start + CMP_BLOCK - 1
    return ((end >= j * SEL_BLOCK) & (start <= j * SEL_BLOCK + SEL_BLOCK - 1)).astype(np.float32)


def nsa_attention(q, q_rot, kc, vc, ks, vs, kw, vw, gates):
    b, g, r, seq, d = q.shape
    n_cmp = kc.shape[2]
    n_blk = seq // SEL_BLOCK
    n_top = min(SEL_TOPN, n_blk)
    scale = d ** -0.5
    qc_n = NSA_Q_CHUNK
    overlap = jnp.asarray(cmp_to_sel_overlap(n_cmp, n_blk))
    ks_blk = ks.reshape(b, g, n_blk, SEL_BLOCK, d)
    vs_blk = vs.reshape(b, g, n_blk, SEL_BLOCK, d)
    kw_pad = jnp.pad(kw, ((0, 0), (0, 0), (WINDOW, 0), (0, 0)))
    vw_pad = jnp.pad(vw, ((0, 0), (0, 0), (WINDOW, 0), (0, 0)))
    cmp_end = jnp.arange(n_cmp) * CMP_STRIDE + CMP_BLOCK - 1
    blk_ids = jnp.arange(n_blk)
    bi = jnp.arange(b)[:, None, None, None]
    gi = jnp.arange(g)[None, :, None, None]

    def chunk(c):
        t0 = c * qc_n
        t = t0 + jnp.arange(qc_n)
        qq = lax.dynamic_slice_in_dim(q, t0, qc_n, axis=3)
        qr = lax.dynamic_slice_in_dim(q_rot, t0, qc_n, axis=3)
        gc = lax.dynamic_slice_in_dim(gates, t0, qc_n, axis=3)
        s_cmp = jnp.einsum('bgrqd,bgnd->bgrqn', qq, kc) * scale
        p_cmp = masked_softmax(s_cmp, cmp_end[None, :] <= t[:, None])
        o_cmp = jnp.einsum('bgrqn,bgnd->bgrqd', p_cmp.astype(vc.dtype), vc)
        imp = jnp.einsum('bgrqn,nj->bgqj', p_cmp, overlap)
        cur = t // SEL_BLOCK
        jj = blk_ids[None, :]
        forced = (jj == 0) | (jj == cur[:, None]) | (jj == cur[:, None] - 1)
        imp = jnp.where(forced, jnp.inf, imp)
        imp = jnp.where(jj <= cur[:, None], imp, -jnp.inf)
        _, top_i = lax.top_k(imp, n_top)
        ksel = ks_blk[bi, gi, top_i]
        vsel = vs_blk[bi, gi, top_i]
        s_sel = jnp.einsum('bgrqd,bgqnld->bgrqnl', qr, ksel) * scale
        kpos = top_i[..., None] * SEL_BLOCK + jnp.arange(SEL_BLOCK)
        m_sel = kpos <= t[:, None, None]
        p_sel = masked_softmax(s_sel.reshape(b, g, r, qc_n, -1),
                               m_sel[:, :, None].reshape(b, g, 1, qc_n, -1))
        o_sel = jnp.einsum('bgrqm,bgqmd->bgrqd', p_sel.astype(vs.dtype),
                           vsel.reshape(b, g, qc_n, -1, d))
        kwin = lax.dynamic_slice_in_dim(kw_pad, t0, WINDOW + qc_n, axis=2)
        vwin = lax.dynamic_slice_in_dim(vw_pad, t0, WINDOW + qc_n, axis=2)
        wpos = t0 - WINDOW + jnp.arange(WINDOW + qc_n)
        m_win = ((wpos[None, :] <= t[:, None]) & (wpos[None, :] > t[:, None] - WINDOW)
                 & (wpos[None, :] >= 0))
        s_win = jnp.einsum('bgrqd,bgkd->bgrqk', qr, kwin) * scale
        p_win = masked_softmax(s_win, m_win)
        o_win = jnp.einsum('bgrqk,bgkd->bgrqd', p_win.astype(vw.dtype), vwin)
        return gc[..., 0:1] * o_cmp + gc[..., 1:2] * o_sel + gc[..., 2:3] * o_win

    out = lax.map(chunk, jnp.arange(seq // qc_n))
    return jnp.transpose(out, (1, 0, 4, 2, 3, 5)).reshape(b, seq, g * r * d)


def moba_attention(q, k, v):
    b, h, seq, d = q.shape
    n_blk = -(-seq // MOBA_BLOCK)
    pad = n_blk * MOBA_BLOCK - seq
    scale = d ** -0.5
    qc_n = MOBA_Q_CHUNK
    kp = jnp.pad(k, ((0, 0), (0, 0), (0, pad), (0, 0)))
    vp = jnp.pad(v, ((0, 0), (0, 0), (0, pad), (0, 0)))
    k_blk = kp.reshape(b, h, n_blk, MOBA_BLOCK, d)
    v_blk = vp.reshape(b, h, n_blk, MOBA_BLOCK, d)
    k_mean = jnp.mean(k_blk.astype(jnp.float32), axis=3).astype(k.dtype)
    n_top = min(MOBA_TOPK, n_blk)
    blk_ids = jnp.arange(n_blk)
    bi = jnp.arange(b)[:, None, None, None]
    hi = jnp.arange(h)[None, :, None, None]

    def chunk(c):
        t0 = c * qc_n
        t = t0 + jnp.arange(qc_n)
        qq = lax.dynamic_slice_in_dim(q, t0, qc_n, axis=2)
        own = t0 // MOBA_BLOCK
        gate = jnp.einsum('bhqd,bhnd->bhqn', qq, k_mean).astype(jnp.float32)
        gate = jnp.where(blk_ids < own, gate, -jnp.inf)
        top_s, top_i = lax.top_k(gate, n_top)
        valid = top_s > -jnp.inf
        kg = k_blk[bi, hi, top_i]
        vg = v_blk[bi, hi, top_i]
        s_past = jnp.einsum('bhqd,bhqkld->bhqkl', qq, kg).reshape(b, h, qc_n, -1) * scale
        m_past = jnp.broadcast_to(valid[..., None], (b, h, qc_n, n_top, MOBA_BLOCK)).reshape(b, h, qc_n, -1)
        k_own = lax.dynamic_slice_in_dim(kp, own * MOBA_BLOCK, MOBA_BLOCK, axis=2)
        v_own = lax.dynamic_slice_in_dim(vp, own * MOBA_BLOCK, MOBA_BLOCK, axis=2)
        s_own = jnp.einsum('bhqd,bhld->bhql', qq, k_own) * scale
        m_own = (own * MOBA_BLOCK + jnp.arange(MOBA_BLOCK))[None, :] <= t[:, None]
        s = jnp.concatenate([s_past, s_own], axis=-1)
        m = jnp.concatenate([m_past, jnp.broadcast_to(m_own, (b, h, qc_n, MOBA_BLOCK))], axis=-1)
        p = masked_softmax(s, m).astype(v.dtype)
        n_past = n_top * MOBA_BLOCK
        return (jnp.einsum('bhqm,bhqmd->bhqd', p[..., :n_past], vg.reshape(b, h, qc_n, -1, d))
                + jnp.einsum('bhql,bhld->bhqd', p[..., n_past:], v_own))

    out = lax.map(chunk, jnp.arange(seq // qc_n))
    return jnp.transpose(out, (1, 0, 3, 2, 4)).reshape(b, seq, h * d)


def setup_inputs(seed: int = 0) -> dict:
    key = jax.random.key(seed)
    ks = jax.random.split(key, 17)
    f32 = jnp.float32

    def nrm(k, shape, fan_in):
        return jax.random.normal(k, shape, f32) * (fan_in ** -0.5)

    def gain(k, shape):
        return 1.0 + 0.01 * jax.random.normal(k, shape, f32)

    L = DEPTH
    return {
        "x": jax.random.normal(ks[0], (BATCH, SEQ, D_MODEL), f32),
        "norm1_g": gain(ks[1], (L, D_MODEL)),
        "w_in": nrm(ks[2], (L, D_MODEL, IN_WIDTH), D_MODEL),
        "cmp_pe_k": 0.02 * jax.random.normal(ks[3], (L, CMP_BLOCK, HEAD_DIM), f32),
        "cmp_pe_v": 0.02 * jax.random.normal(ks[4], (L, CMP_BLOCK, HEAD_DIM), f32),
        "cmp_k_w1": nrm(ks[5], (L, CMP_BLOCK * HEAD_DIM, HEAD_DIM), CMP_BLOCK * HEAD_DIM),
        "cmp_k_w2": nrm(ks[6], (L, HEAD_DIM, HEAD_DIM), HEAD_DIM),
        "cmp_v_w1": nrm(ks[7], (L, CMP_BLOCK * HEAD_DIM, HEAD_DIM), CMP_BLOCK * HEAD_DIM),
        "cmp_v_w2": nrm(ks[8], (L, HEAD_DIM, HEAD_DIM), HEAD_DIM),
        "w_up_nsa": nrm(ks[9], (L, NSA_WIDTH, D_MODEL), NSA_WIDTH),
        "w_up_moba": nrm(ks[10], (L, MOBA_WIDTH, D_MODEL), MOBA_WIDTH),
        "w_out": nrm(ks[11], (L, D_MODEL, D_MODEL), D_MODEL),
        "norm2_g": gain(ks[12], (L, D_MODEL)),
        "w_ff1": nrm(ks[13], (L, D_MODEL, D_FF), D_MODEL),
        "w_ff2": nrm(ks[14], (L, D_FF, D_MODEL), D_FF),
        "norm_f_g": gain(ks[15], (D_MODEL,)),
    }


def reference(x, norm1_g, w_in, cmp_pe_k, cmp_pe_v, cmp_k_w1, cmp_k_w2, cmp_v_w1, cmp_v_w2,
              w_up_nsa, w_up_moba, w_out, norm2_g, w_ff1, w_ff2, norm_f_g):
    b, seq, _ = x.shape
    cos, sin = rope_tables(seq)
    G, R, d = NSA_KV_GROUPS, NSA_GROUP_SIZE, HEAD_DIM
    for layer in range(DEPTH):
        h = rmsnorm(x, norm1_g[layer])
        proj = h @ w_in[layer]
        q_n = proj[..., :OFF_KV]
        kv_n = proj[..., OFF_KV:OFF_GN]
        g_n = proj[..., OFF_GN:OFF_M]
        qkv_m = proj[..., OFF_M:OFF_GM]
        g_m = proj[..., OFF_GM:]

        q = q_n.reshape(b, seq, NSA_HEADS, d)
        q_r = partial_rope(q, cos, sin)
        kv = kv_n.reshape(b, seq, 6, G, d)
        k_c, v_c, k_s, v_s, k_w, v_w = (kv[:, :, i] for i in range(6))
        k_s = partial_rope(k_s, cos, sin)
        k_w = partial_rope(k_w, cos, sin)
        kc = compress_blocks(k_c, cmp_pe_k[layer], cmp_k_w1[layer], cmp_k_w2[layer])
        vc = compress_blocks(v_c, cmp_pe_v[layer], cmp_v_w1[layer], cmp_v_w2[layer])
        to_q = lambda t: jnp.transpose(t.reshape(b, seq, G, R, d), (0, 2, 3, 1, 4))
        to_kv = lambda t: jnp.transpose(t, (0, 2, 1, 3))
        gates = jnp.transpose(
            jax.nn.sigmoid(g_n.astype(jnp.float32)).reshape(b, seq, G, R, 3),
            (0, 2, 3, 1, 4)).astype(x.dtype)
        y_nsa = nsa_attention(to_q(q), to_q(q_r), kc, vc, to_kv(k_s), to_kv(v_s),
                              to_kv(k_w), to_kv(v_w), gates)

        qkv = qkv_m.reshape(b, seq, 3, MOBA_HEADS, d)
        qm = partial_rope(qkv[:, :, 0], cos, sin)
        km = partial_rope(qkv[:, :, 1], cos, sin)
        vm = qkv[:, :, 2]
        y_moba = moba_attention(to_kv(qm), to_kv(km), to_kv(vm))

        gm = jax.nn.sigmoid(g_m.astype(jnp.float32)).astype(x.dtype)
        g_nsa = gm[..., :D_MODEL]
        g_moba = gm[..., D_MODEL:]
        mixed = g_nsa * (y_nsa @ w_up_nsa[layer]) + g_moba * (y_moba @ w_up_moba[layer])
        x = x + mixed @ w_out[layer]

        h2 = rmsnorm(x, norm2_g[layer])
        u = jax.nn.relu(h2 @ w_ff1[layer])
        x = x + (u * u) @ w_ff2[layer]
    return rmsnorm(x, norm_f_g)
```

```python
from contextlib import ExitStack
import numpy as np
import concourse.bass as bass
import concourse.mybir as mybir
from concourse.bass_utils import run_bass_kernel_spmd

F32 = mybir.dt.float32
BF16 = mybir.dt.bfloat16
ALU = mybir.AluOpType
AF = mybir.ActivationFunctionType
AX = mybir.AxisListType

S = 4096
D = 1024
NB = 16
BT = 256
NEG = -30000.0
BIG = 1.0e30
OFF_KV = 512
OFF_GN = 1280
OFF_M = 1304
OFF_GM = 2840
IN_W = 4888
EPS = 1e-6


class Sched:
    def __init__(self, nc, ctx, n_dma_sems=8):
        self.nc = nc
        self.eng = {"pe": nc.tensor, "act": nc.scalar, "dve": nc.vector, "pool": nc.gpsimd, "sp": nc.sync}
        self.sem = {}
        self.cnt = {}
        for k in self.eng:
            self.sem[k] = ctx.enter_context(nc.semaphore("s_" + k))
            self.cnt[k] = 0
        self.dsem = {}
        self.dval = {}
        self.drr = {}
        for q in ("sp", "pool"):
            self.dsem[q] = [ctx.enter_context(nc.semaphore(f"d_{q}{i}")) for i in range(n_dma_sems)]
            self.dval[q] = [0] * n_dma_sems
            self.drr[q] = 0
        self.seen = {k: {} for k in self.eng}
        self.lastw = {}
        self.reads = {}
        self.tilekeys = {}
        self.semobj = {}
        for k in self.eng:
            self.semobj[("e", k)] = self.sem[k]
        for q in self.dsem:
            for i, s_ in enumerate(self.dsem[q]):
                self.semobj[("d", q, i)] = s_
        self.ninst = 0
        self.ps_last = {}
        self.pe_last = {}

    @staticmethod
    def _norm(k):
        if isinstance(k, tuple):
            return (k[0], k[1])
        return (k, None)

    def _conf(self, key):
        t, sub = key
        if sub is None:
            return list(self.tilekeys.get(t, ()))
        return [(t, None), (t, sub)]

    def _need(self, R, W):
        need = {}

        def add(tok):
            if tok is None:
                return
            sk, v = tok
            if need.get(sk, 0) < v:
                need[sk] = v

        for r in R:
            for k in self._conf(self._norm(r)):
                add(self.lastw.get(k))
        for w in W:
            for k in self._conf(self._norm(w)):
                add(self.lastw.get(k))
                for tok in self.reads.get(k, ()):
                    add(tok)
        return need

    def _waits(self, me, need):
        e = self.eng[me]
        for sk, v in need.items():
            if me == "pe" and sk == ("e", "pe"):
                continue
            if self.seen[me].get(sk, 0) >= v:
                continue
            e.wait_ge(self.semobj[sk], v)
            self.seen[me][sk] = v

    def _record(self, tok, R, W):
        for w in W:
            w = self._norm(w)
            ks = self.tilekeys.setdefault(w[0], set())
            ks.add(w)
            ks.add((w[0], None))
            if w[1] is None:
                for k in ks:
                    self.lastw[k] = tok
                    self.reads[k] = []
            else:
                self.lastw[w] = tok
                self.reads[w] = []
        for r in R:
            r = self._norm(r)
            ks = self.tilekeys.setdefault(r[0], set())
            ks.add(r)
            ks.add((r[0], None))
            lst = self.reads.setdefault(r, [])
            lst.append(tok)
            if len(lst) > 16:
                d = {}
                for sk, v in lst:
                    d[sk] = max(d.get(sk, 0), v)
                self.reads[r] = list(d.items())

    def op(self, me, f, R=(), W=(), rg=(0, 128)):
        need = self._need(R, W)
        if me == "pe":
            grp = set(range(rg[0] // 32, (rg[0] + rg[1] + 31) // 32))
            for k_ in W:
                t_ = self._norm(k_)[0]
                prev = self.pe_last.get(t_)
                if prev is not None and not (prev[0] & grp):
                    if self.seen["pe"].get(("e", "pe"), 0) < prev[1]:
                        self.eng["pe"].wait_ge(self.sem["pe"], prev[1])
                        self.seen["pe"][("e", "pe")] = prev[1]
                self.pe_last[t_] = (grp, self.cnt["pe"] + 1)
        banks = set()
        for k_ in list(R) + list(W):
            t_ = self._norm(k_)[0]
            if isinstance(t_, str) and t_.startswith("ps"):
                banks.add(t_)
        for bk in banks:
            for eng_, tok in self.ps_last.get(bk, {}).items():
                if eng_ != me:
                    sk, v = tok
                    if need.get(sk, 0) < v:
                        need[sk] = v
        self._waits(me, need)
        ins = f(self.eng[me])
        self.cnt[me] += 1
        ins.then_inc(self.sem[me], 1)
        for bk in banks:
            self.ps_last.setdefault(bk, {})[me] = (("e", me), self.cnt[me])
        self._record((("e", me), self.cnt[me]), R, W)
        self.ninst += 1
        return ins

    def dma(self, q, out, in_, R=(), W=()):
        need = self._need(R, W)
        i = self.drr[q]
        self.drr[q] = (i + 1) % len(self.dsem[q])
        sk = ("d", q, i)
        if self.dval[q][i] > 0:
            need[sk] = max(need.get(sk, 0), self.dval[q][i])
        self._waits(q, need)
        ins = self.eng[q].dma_start(out=out, in_=in_)
        self.dval[q][i] += 16
        ins.then_inc(self.dsem[q][i], 16)
        self._record((sk, self.dval[q][i]), R, W)
        self.ninst += 1
        return ins

    def finish(self, keys, me="sp"):
        self._waits(me, self._need(keys, ()))


def make_consts():
    c = {}
    inv = 500000.0 ** (-np.arange(0, 16, 2, dtype=np.float32) / 16.0)
    ang = np.arange(S, dtype=np.float32)[:, None] * inv[None, :].astype(np.float32)
    ang = ang.astype(np.float32)
    cos = np.cos(ang).astype(np.float32)
    sin = np.sin(ang).astype(np.float32)
    c["c_cs"] = np.concatenate([cos, cos], axis=1).astype(np.float32)
    c["c_sn"] = np.concatenate([-sin, sin], axis=1).astype(np.float32)
    F0 = np.zeros((128, 127), np.float32)
    for p in range(128):
        cur = 0 if p < 64 else 1
        for cc in range(127):
            jp = cc - 63
            if jp == cur or jp == cur - 1:
                F0[p, cc] = BIG
            elif jp > cur:
                F0[p, cc] = -BIG
    c["c_f0"] = F0
    p = np.arange(128)[:, None]
    cc = np.arange(2176)[None, :]
    c["c_tc"] = np.where(16 * p + 15 <= cc, 0.0, NEG).astype(np.float32)
    k = np.arange(128)[:, None]
    q = np.arange(128)[None, :]
    c["c_caus"] = np.where(k <= q, 0.0, NEG).astype(np.float32)
    c["c_caus2"] = np.where(k > q, 0.0, NEG).astype(np.float32)
    j = np.arange(64)[:, None]
    kk = np.arange(S)[None, :]
    c["c_esel"] = (kk // 64 == j).astype(np.float32)
    em = np.zeros((96, 16, 128), np.float32)
    for s_ in range(3):
        for b in range(16):
            em[32 * s_ + b, b, :] = 1.0
    c["c_em"] = em.reshape(96, 16 * 128)
    ovl = np.zeros((256, 65), np.float32)
    for npp in range(1, 256):
        n = npp - 1
        start = n * 16
        end = start + 31
        for jb in range(64):
            if end >= jb * 64 and start <= jb * 64 + 63:
                ovl[npp, jb] = 1.0
        ovl[npp, 64] = 1.0
    c["c_ovl"] = ovl.reshape(2, 128, 65).transpose(1, 0, 2).reshape(128, 130).copy()
    own = np.arange(16)[:, None]
    blk = np.arange(16)[None, :]
    mv = np.where(blk < own, 0.0, -BIG).astype(np.float32)
    va = (blk < own).astype(np.float32)
    ow = (blk == own).astype(np.float32)
    c["c_mv"] = np.stack([mv, va, ow], 0).reshape(1, 3 * 256).repeat(128, 0).copy()
    return {k_: np.ascontiguousarray(v, dtype=np.float32) for k_, v in c.items()}


class _Stop(Exception):
    pass


def build_nc(debug=False, nb_run=NB, branch_only=None, stop=99):
    nc = bass.Bass("TRN2", target_bir_lowering=False)
    dt_in = lambda name, shape: nc.dram_tensor(name, shape, F32, kind="ExternalInput").ap()
    x = dt_in("x", [S, D])
    norm1_g = dt_in("norm1_g", [1, D])
    w_in = dt_in("w_in", [D, IN_W])
    pe_k = dt_in("cmp_pe_k", [32, 64])
    pe_v = dt_in("cmp_pe_v", [32, 64])
    k_w1 = dt_in("cmp_k_w1", [2048, 64])
    k_w2 = dt_in("cmp_k_w2", [64, 64])
    v_w1 = dt_in("cmp_v_w1", [2048, 64])
    v_w2 = dt_in("cmp_v_w2", [64, 64])
    w_un = dt_in("w_up_nsa", [512, D])
    w_um = dt_in("w_up_moba", [512, D])
    w_out = dt_in("w_out", [D, D])
    norm2_g = dt_in("norm2_g", [1, D])
    w_ff1 = dt_in("w_ff1", [D, 4096])
    w_ff2 = dt_in("w_ff2", [4096, D])
    normf_g = dt_in("norm_f_g", [1, D])
    c_cs = dt_in("c_cs", [S, 16])
    c_sn = dt_in("c_sn", [S, 16])
    c_f0 = dt_in("c_f0", [128, 127])
    c_tc = dt_in("c_tc", [128, 2176])
    c_caus = dt_in("c_caus", [128, 128])
    c_caus2 = dt_in("c_caus2", [128, 128])
    c_esel = dt_in("c_esel", [64, S])
    c_em = dt_in("c_em", [96, 2048])
    c_ovl = dt_in("c_ovl", [128, 130])
    c_mv = dt_in("c_mv", [128, 768])
    y = nc.dram_tensor("y", [S, D], F32, kind="ExternalOutput").ap()
    kmd = nc.dram_tensor("kmd", [32, 128, 512], BF16, kind="Internal").ap()
    vmd = nc.dram_tensor("vmd", [32, 128, 520], BF16, kind="Internal").ap()
    if debug:
        dbg_yn = nc.dram_tensor("dbg_yn", [S, 512], F32, kind="ExternalOutput").ap()
        dbg_ym = nc.dram_tensor("dbg_ym", [S, 512], F32, kind="ExternalOutput").ap()
        dbg_x1 = nc.dram_tensor("dbg_x1", [S, D], F32, kind="ExternalOutput").ap()

    with ExitStack() as ctx:
        s = Sched(nc, ctx)
        T = lambda name, shape, dt: ctx.enter_context(nc.sbuf_tensor(name, shape, dt))
        PS = lambda name, shape, dt: ctx.enter_context(nc.psum_tensor(name, shape, dt))

        psG = [PS(f"psg{i}", [128, 512], F32) for i in range(3)]
        psT = PS("pst", [128, 1024], BF16)
        psO = PS("pso", [128, 512], F32)
        psO2 = PS("pso2", [128, 512], F32)
        psM = [PS(f"psm{i}", [128, 512], F32) for i in range(2)]
        gen_state = {"i": 0}

        def gps(pool=None):
            lst = pool if pool is not None else psG
            t = lst[gen_state["i"] % len(lst)]
            gen_state["i"] += 1
            return t
        allps = psG + [psO, psO2] + psM

        ident_f = T("ident_f", [128, 128], F32)
        ident_b = T("ident_b", [128, 128], BF16)
        CS = T("CS", [128, 32, 16], F32)
        SN = T("SN", [128, 32, 16], F32)
        F0 = T("F0", [128, 127], F32)
        TC = T("TC", [128, 2176], BF16)
        CAUS = T("CAUS", [128, 128], BF16)
        CAUS2 = T("CAUS2", [128, 128], BF16)
        CAUSW = T("CAUSW", [128, 256], BF16)
        ESEL = T("ESEL", [128, S], BF16)
        EM = T("EM", [96, 16, 128], BF16)
        OVL = T("OVL", [128, 2, 65], BF16)
        MV = T("MV", [128, 3, 16, 16], F32)
        G1 = T("G1", [128, D], F32)
        G2 = T("G2", [128, D], F32)
        GF = T("GF", [128, D], F32)
        W1K = T("W1K", [128, 32, 64], BF16)
        W1V = T("W1V", [128, 32, 64], BF16)
        W2KP = T("W2KP", [64, 2, 128], BF16)
        W2V = T("W2V", [64, 64], BF16)
        PEs = T("PEs", [32, 2, 64], BF16)
        PET = T("PET", [128, 2, 32], BF16)
        CB = T("CB", [64, 2], F32)
        KsT = T("KsT", [128, S], BF16)
        KwT = T("KwT", [128, S], BF16)
        VsA = T("VsA", [128, 32, 2, 65], BF16)
        VwA = T("VwA", [128, 32, 2, 65], BF16)
        KCT = T("KCT", [128, 256], BF16)
        VC = T("VC", [128, 2, 2, 65], BF16)
        KMT = T("KMT", [128, 4, 16], BF16)
        KcR = T("KcR", [128, 16 + BT], BF16)
        VcR = T("VcR", [128, 16 + BT], BF16)
        NSLAB = 3
        slabs = [T(f"slab{i}", [128, 4096], BF16) for i in range(NSLAB)]
        slab_state = {"i": 0}

        def next_slab():
            i = slab_state["i"] % NSLAB
            slab_state["i"] += 1
            return slabs[i], f"slab{i}"

        xbuf = T("xbuf", [128, 2, D], F32)
        h_tm = T("h_tm", [128, D], BF16)
        ss = T("ss", [128, 4], F32)
        hT = T("hT", [128, 8, BT], BF16)
        tmA = T("tmA", [128, 512], BF16)
        tmB = T("tmB", [128, 512], BF16)
        ropa = T("ropa", [128, 8, 16], F32)
        ropb = T("ropb", [128, 8, 16], F32)
        QnT = T("QnT", [128, 4, BT], BF16)
        QrT = T("QrT", [128, 4, BT], BF16)
        QmT = T("QmT", [128, 4, BT], BF16)
        KmB = T("KmB", [128, 4, BT], BF16)
        VmB = T("VmB", [128, 2, 8, 65], BF16)
        MmT = T("MmT", [96, 4, BT], BF16)
        gates = T("gates", [128, 2, 24], F32)
        gmv = T("gmv", [128, 8, 16], F32)
        gmk = T("gmk", [128, 8, 16], F32)
        mx8 = T("mx8", [128, 8, 8], F32)
        mnegp = T("mnegp", [128, 4, 3, 32], BF16)
        kms = T("kms", [128, 4], F32)
        cx = T("cx", [64, 4, 16], F32)
        cx2 = T("cx2", [64, 4, 16], F32)
        csg = T("csg", [64, 4, 16], F32)
        cg = T("cg", [64, 4, 16], BF16)
        vcs = T("vcs", [16, 2, 65], BF16)
        pT = [T(f"pT{i}", [128, 512], BF16) for i in range(3)]
        pc = [T(f"pc{i}", [128, 512], BF16) for i in range(2)]
        OTs = T("OTs", [128, 512], F32)
        rs4 = T("rs4", [128, 4], F32)
        rc4 = T("rc4", [128, 4], F32)
        cf4 = T("cf4", [128, 4], F32)
        imp = T("imp", [128, 64], F32)
        imp2 = T("imp2", [128, 64], F32)
        m8a = T("m8a", [128, 8], F32)
        m8b = T("m8b", [128, 8], F32)
        mselb = T("mselb", [128, 64], BF16)
        MselT = T("MselT", [128, 128], BF16)
        yacc = T("yacc", [128, 4, 64], F32)
        ytmp = T("ytmp", [128, 4, 64], F32)
        yn_tm = T("yn_tm", [128, 2, 512], BF16)
        ym_tm = T("ym_tm", [128, 2, 512], BF16)
        kring = [T(f"kring{i}", [128, 4, 128], BF16) for i in range(3)]
        vring = [T(f"vring{i}", [128, 8, 65], BF16) for i in range(3)]
        ynT = T("ynT", [128, 4, BT], BF16)
        ymT = T("ymT", [128, 4, BT], BF16)
        sg = [T(f"sg{i}", [128, BT], F32) for i in range(4)]
        mixT = T("mixT", [128, 8, BT], BF16)
        x1 = T("x1", [128, 2, D], F32)
        uT = T("uT", [128, 16, BT], BF16)
        rl = [T(f"rl{i}", [128, BT], F32) for i in range(2)]
        stg = [T(f"stg{i}", [128, BT], F32) for i in range(2)]

        wun_t = T("wun_t", [128, 4, D], BF16)
        wum_t = T("wum_t", [128, 4, D], BF16)
        print("sbuf bytes remaining:", nc.sbuf_bytes_remaining)
        P_ = lambda f, R=(), W=(): s.op("pool", f, R, W)
        Dv = lambda f, R=(), W=(): s.op("dve", f, R, W)
        Ac = lambda f, R=(), W=(): s.op("act", f, R, W)
        Pe = lambda f, R=(), W=(), rg=(0, 128): s.op("pe", f, R, W, rg)

        P_(lambda e: e.memset(ident_f[:], 1.0), W=["ident_f"])
        P_(lambda e: e.affine_select(out=ident_f[:], in_=ident_f[:], pattern=[[-1, 128]], compare_op=ALU.is_equal,
                                     fill=0.0, base=0, channel_multiplier=1), R=["ident_f"], W=["ident_f"])
        Dv(lambda e: e.tensor_copy(out=ident_b[:], in_=ident_f[:]), R=["ident_f"], W=["ident_b"])
        s.dma("sp", CS[:], c_cs.rearrange("(t p) e -> p t e", p=128), W=["CS"])
        s.dma("sp", SN[:], c_sn.rearrange("(t p) e -> p t e", p=128), W=["SN"])
        s.dma("sp", F0[:], c_f0, W=["F0"])
        s.dma("sp", MV[:].rearrange("p a b c -> p (a b c)"), c_mv, W=["MV"])
        s.dma("sp", G1[:], norm1_g[0:1, :].to_broadcast([128, D]), W=["G1"])
        s.dma("sp", G2[:], norm2_g[0:1, :].to_broadcast([128, D]), W=["G2"])
        s.dma("sp", GF[:], normf_g[0:1, :].to_broadcast([128, D]), W=["GF"])
        s.dma("pool", TC[:], c_tc, W=["TC"])
        s.dma("pool", CAUS[:], c_caus, W=["CAUS"])
        s.dma("pool", CAUS2[:], c_caus2, W=["CAUS2"])
        P_(lambda e: e.memset(CAUSW[:], 0.0), W=["CAUSW"])
        s.dma("pool", CAUSW[:, 0:128], c_caus, R=["CAUSW"], W=["CAUSW"])
        s.dma("pool", ESEL[0:64, :], c_esel, W=["ESEL"])
        s.dma("pool", ESEL[64:128, :], c_esel, W=["ESEL"])
        s.dma("pool", EM[:].rearrange("p a b -> p (a b)"), c_em, W=["EM"])
        s.dma("pool", OVL[:].rearrange("p a b -> p (a b)"), c_ovl, W=["OVL"])
        for half in range(2):
            s.dma("pool", W1K[half * 64:(half + 1) * 64, :, :], k_w1.rearrange("(l d) j -> d l j", d=64), W=["W1K"])
            s.dma("pool", W1V[half * 64:(half + 1) * 64, :, :], v_w1.rearrange("(l d) j -> d l j", d=64), W=["W1V"])
        P_(lambda e: e.memset(W2KP[:], 0.0), W=["W2KP"])
        s.dma("pool", W2KP[:, 0, 0:64], k_w2, R=["W2KP"], W=["W2KP"])
        s.dma("pool", W2KP[:, 1, 64:128], k_w2, R=["W2KP"], W=["W2KP"])
        s.dma("pool", W2V[:], v_w2, W=["W2V"])
        s.dma("pool", PEs[:, 0, :], pe_k, W=["PEs"])
        s.dma("pool", PEs[:, 1, :], pe_v, R=["PEs"], W=["PEs"])
        for t_, key_, val in ((VsA, "VsA", 1.0), (VwA, "VwA", 1.0), (VmB, "VmB", 1.0), (vcs, "vcs", 1.0), (VC, "VC", 0.0), (KCT, "KCT", 0.0),
                              (KMT, "KMT", 0.0), (KcR, "KcR", 0.0), (VcR, "VcR", 0.0), (gmv, "gmv", 0.0), (gmk, "gmk", 0.0),
                              (mnegp, "mnegp", 0.0), (mx8, "mx8", 0.0), (ss, "ss", 0.0)):
            ap_ = t_[:]
            shp = list(ap_.shape)
            flat = ap_
            if len(shp) == 3:
                flat = ap_.rearrange("p a b -> p (a b)")
            elif len(shp) == 4:
                flat = ap_.rearrange("p a b c -> p (a b c)")
            P_(lambda e, flat=flat, val=val: e.memset(flat, val), W=[key_])
        for kv in range(2):
            Pe(lambda e, kv=kv: e.transpose(out=psT[0:64, kv * 32:(kv + 1) * 32], in_=PEs[:, kv, :], identity=ident_b[0:32, 0:32]),
               R=["PEs", "ident_b"], W=["psT"], rg=(0, 32))
        Dv(lambda e: e.tensor_copy(out=PET[0:64, :, :], in_=psT[0:64, 0:64].rearrange("p (a b) -> p a b", a=2)), R=["psT"], W=["PET"])
        pb = gps()
        for kv, W1 in enumerate((W1K, W1V)):
            for l in range(32):
                Pe(lambda e, kv=kv, W1=W1, l=l: e.matmul(pb[0:64, kv:kv + 1], lhsT=W1[0:64, l, :], rhs=PET[0:64, kv, l:l + 1],
                                                         start=(l == 0), stop=(l == 31)),
                   R=["W1K", "W1V", "PET"], W=[pb.name], rg=(0, 64))
        Dv(lambda e: e.tensor_copy(out=CB[:], in_=pb[0:64, 0:2]), R=[pb.name], W=["CB"])

        def rmsnorm_tile(src_ap, src_keys, gtab, gkey, ssi, out_ap, out_keys):
            Ac(lambda e: e.activation(out=h_tm[:], in_=src_ap, func=AF.Square, accum_out=ss[:, ssi:ssi + 1]),
               R=src_keys, W=["h_tm", ("ss", ssi)])
            Dv(lambda e: e.tensor_scalar(out=ss[:, ssi:ssi + 1], in0=ss[:, ssi:ssi + 1], scalar1=1.0 / D, scalar2=EPS,
                                         op0=ALU.mult, op1=ALU.add), R=[("ss", ssi)], W=[("ss", ssi)])
            Ac(lambda e: e.activation(out=ss[:, ssi:ssi + 1], in_=ss[:, ssi:ssi + 1], func=AF.Sqrt), R=[("ss", ssi)], W=[("ss", ssi)])
            Dv(lambda e: e.reciprocal(out=ss[:, ssi:ssi + 1], in_=ss[:, ssi:ssi + 1]), R=[("ss", ssi)], W=[("ss", ssi)])
            Dv(lambda e: e.scalar_tensor_tensor(out=out_ap, in0=src_ap, scalar=ss[:, ssi:ssi + 1], in1=gtab[:],
                                                op0=ALU.mult, op1=ALU.mult), R=list(src_keys) + [("ss", ssi), gkey], W=out_keys)
            Dv(lambda e: e.memset(ss[:, ssi:ssi + 1], 0.0), R=[("ss", ssi)], W=[("ss", ssi)])

        def to_hT(tt):
            for kc in range(8):
                Pe(lambda e, kc=kc: e.transpose(out=psT[:, kc * 128:(kc + 1) * 128], in_=h_tm[:, kc * 128:(kc + 1) * 128],
                                                identity=ident_b[:]), R=["h_tm", "ident_b"], W=["psT"])
            Ac(lambda e: e.activation(out=hT[:, :, tt * 128:(tt + 1) * 128], in_=psT[:, :].rearrange("p (a b) -> p a b", a=8),
                                      func=AF.Copy), R=["psT"], W=[("hT", tt)])

        def load_w_slab(src_ap, ncols):
            sl, key = next_slab()
            s.dma("pool", sl[:, 0:8 * ncols].rearrange("p (a b) -> p a b", a=8), src_ap, W=[key])
            return sl, key

        def rope(ps_v, out_v, ti, nh, R, W):
            csb = CS[:, ti, :].unsqueeze(1).to_broadcast([128, nh, 16])
            Dv(lambda e: e.tensor_copy(out=out_v, in_=ps_v), R=R, W=W)
            Dv(lambda e: e.tensor_tensor(out=ropa[:, 0:nh, :], in0=ps_v[:, :, 0:16], in1=csb, op=ALU.mult), R=R + ["CS"], W=["ropa"])
            Dv(lambda e: e.tensor_tensor(out=ropb[:, 0:nh, 0:8], in0=ps_v[:, :, 8:16],
                                         in1=SN[:, ti, 0:8].unsqueeze(1).to_broadcast([128, nh, 8]), op=ALU.mult), R=R + ["SN"], W=[("ropb", 0)])
            Dv(lambda e: e.tensor_tensor(out=ropb[:, 0:nh, 8:16], in0=ps_v[:, :, 0:8],
                                         in1=SN[:, ti, 8:16].unsqueeze(1).to_broadcast([128, nh, 8]), op=ALU.mult), R=R + ["SN"], W=[("ropb", 1)])
            return

        def rope_fin(out_rot_v, nh, W):
            Dv(lambda e: e.tensor_tensor(out=out_rot_v, in0=ropa[:, 0:nh, :], in1=ropb[:, 0:nh, :], op=ALU.add),
               R=["ropa", "ropb"], W=W)

        def run_block(b):
            tok0 = b * BT
            for tt in range(2):
                r0 = tok0 + tt * 128
                s.dma("sp", xbuf[:, tt, :], x[r0:r0 + 128, :], W=[("xbuf", tt)])
                rmsnorm_tile(xbuf[:, tt, :], [("xbuf", tt)], G1, "G1", tt, h_tm[:], ["h_tm"])
                to_hT(tt)

            chunks = [(0, 512), (512, 512), (1024, 280), (OFF_M, 512), (OFF_M + 512, 512), (OFF_M + 1024, 512)]
            for ci, (c0, ncol) in enumerate(chunks):
                sl, skey = load_w_slab(w_in[:, c0:c0 + ncol].rearrange("(kc p) n -> p kc n", p=128), ncol)
                slv = sl[:, 0:8 * ncol].rearrange("p (a b) -> p a b", a=8)
                for tt in range(2):
                    ti = 2 * b + tt
                    ps = gps()
                    for kc in range(8):
                        Pe(lambda e, kc=kc, tt=tt, ps=ps: e.matmul(ps[:, 0:ncol], lhsT=hT[:, kc, tt * 128:(tt + 1) * 128], rhs=slv[:, kc, :],
                                                                 start=(kc == 0), stop=(kc == 7)), R=[("hT", tt), skey], W=[ps.name])
                    pk = [ps.name]
                    if ci == 0:
                        psv = ps[:, 0:512].rearrange("p (g r d) -> p g r d", g=2, r=4)
                        Ac(lambda e, psv=psv: e.activation(out=tmA[:, :].rearrange("p (r g d) -> p g r d", r=4, g=2), in_=psv, func=AF.Copy),
                           R=pk, W=["tmA"])
                        ps8 = ps[:, 0:512].rearrange("p (h d) -> p h d", h=8)
                        Dv(lambda e, psv=psv: e.tensor_copy(out=tmB[:, :].rearrange("p (r g d) -> p g r d", r=4, g=2), in_=psv), R=pk, W=["tmB"])
                        csb = CS[:, ti, :].unsqueeze(1).to_broadcast([128, 8, 16])
                        Dv(lambda e, ps8=ps8, csb=csb: e.tensor_tensor(out=ropa[:, :, :], in0=ps8[:, :, 0:16], in1=csb, op=ALU.mult), R=pk + ["CS"], W=["ropa"])
                        Dv(lambda e, ps8=ps8, ti=ti: e.tensor_tensor(out=ropb[:, :, 0:8], in0=ps8[:, :, 8:16],
                                                                     in1=SN[:, ti, 0:8].unsqueeze(1).to_broadcast([128, 8, 8]), op=ALU.mult), R=pk + ["SN"], W=[("ropb", 0)])
                        Dv(lambda e, ps8=ps8, ti=ti: e.tensor_tensor(out=ropb[:, :, 8:16], in0=ps8[:, :, 0:8],
                                                                     in1=SN[:, ti, 8:16].unsqueeze(1).to_broadcast([128, 8, 8]), op=ALU.mult), R=pk + ["SN"], W=[("ropb", 1)])
                        Dv(lambda e: e.tensor_tensor(out=tmB[:, :].rearrange("p (r g d) -> p g r d", r=4, g=2)[:, :, :, 0:16],
                                                     in0=ropa[:, :, :].rearrange("p (g r) e -> p g r e", g=2),
                                                     in1=ropb[:, :, :].rearrange("p (g r) e -> p g r e", g=2), op=ALU.add),
                           R=["ropa", "ropb", "tmB"], W=["tmB"])
                        for r in range(4):
                            Pe(lambda e, r=r: e.transpose(out=psT[:, r * 128:(r + 1) * 128], in_=tmA[:, r * 128:(r + 1) * 128], identity=ident_b[:]),
                               R=["tmA", "ident_b"], W=["psT"])
                            Pe(lambda e, r=r: e.transpose(out=psT[:, 512 + r * 128:512 + (r + 1) * 128], in_=tmB[:, r * 128:(r + 1) * 128], identity=ident_b[:]),
                               R=["tmB", "ident_b"], W=["psT"])
                        Ac(lambda e, tt=tt: e.activation(out=QnT[:, :, tt * 128:(tt + 1) * 128], in_=psT[:, 0:512].rearrange("p (a b) -> p a b", a=4), func=AF.Copy),
                           R=["psT"], W=[("QnT", tt)])
                        Dv(lambda e, tt=tt: e.tensor_copy(out=QrT[:, :, tt * 128:(tt + 1) * 128], in_=psT[:, 512:1024].rearrange("p (a b) -> p a b", a=4)),
                           R=["psT"], W=[("QrT", tt)])
                    elif ci == 1:
                        Ac(lambda e, ps=ps: e.activation(out=tmA[:, 0:256], in_=ps[:, 0:256], func=AF.Copy), R=pk, W=["tmA"])
                        ps2 = ps[:, 256:384].rearrange("p (h d) -> p h d", h=2)
                        tv = tmB[:, 0:128].rearrange("p (h d) -> p h d", h=2)
                        rope(ps2, tv, ti, 2, pk, ["tmB"])
                        rope_fin(tv[:, :, 0:16], 2, ["tmB"])
                        Ac(lambda e, ps=ps, ti=ti: e.activation(out=VsA[:, ti, :, 0:64], in_=ps[:, 384:512].rearrange("p (g d) -> p g d", g=2), func=AF.Copy),
                           R=pk, W=[("VsA", ti)])
                        Pe(lambda e: e.transpose(out=psT[:, 0:128], in_=tmA[:, 0:128], identity=ident_b[:]), R=["tmA", "ident_b"], W=["psT"])
                        Pe(lambda e: e.transpose(out=psT[:, 128:256], in_=tmA[:, 128:256], identity=ident_b[:]), R=["tmA", "ident_b"], W=["psT"])
                        Pe(lambda e: e.transpose(out=psT[:, 256:384], in_=tmB[:, 0:128], identity=ident_b[:]), R=["tmB", "ident_b"], W=["psT"])
                        Dv(lambda e, tt=tt: e.tensor_copy(out=KcR[:, 16 + tt * 128:16 + (tt + 1) * 128], in_=psT[:, 0:128]), R=["psT"], W=["KcR"])
                        Dv(lambda e, tt=tt: e.tensor_copy(out=VcR[:, 16 + tt * 128:16 + (tt + 1) * 128], in_=psT[:, 128:256]), R=["psT"], W=["VcR"])
                        Ac(lambda e, ti=ti: e.activation(out=KsT[:, ti * 128:(ti + 1) * 128], in_=psT[:, 256:384], func=AF.Copy), R=["psT"], W=[("KsT", ti)])
                    elif ci == 2:
                        ps2 = ps[:, 0:128].rearrange("p (h d) -> p h d", h=2)
                        tv = tmB[:, 0:128].rearrange("p (h d) -> p h d", h=2)
                        rope(ps2, tv, ti, 2, pk, ["tmB"])
                        rope_fin(tv[:, :, 0:16], 2, ["tmB"])
                        Ac(lambda e, ps=ps, ti=ti: e.activation(out=VwA[:, ti, :, 0:64], in_=ps[:, 128:256].rearrange("p (g d) -> p g d", g=2), func=AF.Copy),
                           R=pk, W=[("VwA", ti)])
                        Ac(lambda e, ps=ps, tt=tt: e.activation(out=gates[:, tt, :], in_=ps[:, 256:280], func=AF.Sigmoid), R=pk, W=[("gates", tt)])
                        Pe(lambda e: e.transpose(out=psT[:, 0:128], in_=tmB[:, 0:128], identity=ident_b[:]), R=["tmB", "ident_b"], W=["psT"])
                        Dv(lambda e, ti=ti: e.tensor_copy(out=KwT[:, ti * 128:(ti + 1) * 128], in_=psT[:, 0:128]), R=["psT"], W=[("KwT", ti)])
                    elif ci in (3, 4):
                        ps8 = ps[:, 0:512].rearrange("p (h d) -> p h d", h=8)
                        tv = tmA[:, :].rearrange("p (h d) -> p h d", h=8)
                        rope(ps8, tv, ti, 8, pk, ["tmA"])
                        rope_fin(tv[:, :, 0:16], 8, ["tmA"])
                        for pr in range(4):
                            Pe(lambda e, pr=pr: e.transpose(out=psT[:, pr * 128:(pr + 1) * 128], in_=tmA[:, pr * 128:(pr + 1) * 128], identity=ident_b[:]),
                               R=["tmA", "ident_b"], W=["psT"])
                        dst = QmT if ci == 3 else KmB
                        dk = "QmT" if ci == 3 else "KmB"
                        Ac(lambda e, tt=tt, dst=dst: e.activation(out=dst[:, :, tt * 128:(tt + 1) * 128], in_=psT[:, 0:512].rearrange("p (a b) -> p a b", a=4), func=AF.Copy),
                           R=["psT"], W=[(dk, tt)])
                        if ci == 4:
                            s.dma("sp", kmd[ti].rearrange("p (a b) -> p a b", a=4), KmB[:, :, tt * 128:(tt + 1) * 128], R=[("KmB", tt)], W=[("kmd", ti)])
                    else:
                        Ac(lambda e, ps=ps, tt=tt: e.activation(out=VmB[:, tt, :, 0:64], in_=ps[:, 0:512].rearrange("p (h d) -> p h d", h=8), func=AF.Copy),
                           R=pk, W=[("VmB", tt)])
                        s.dma("sp", vmd[ti].rearrange("p (a b) -> p a b", a=8), VmB[:, tt, :, :], R=[("VmB", tt)], W=[("vmd", ti)])

            if stop < 2:
                raise _Stop()
            for tt in range(2):
                pg = gps()
                for h in (0, 2, 4, 6, 1, 3, 5, 7):
                    pr, base = h // 2, 64 * (h % 2)
                    Pe(lambda e, h=h, pr=pr, base=base, tt=tt, pg=pg: e.matmul(pg[:, h * 16:(h + 1) * 16], lhsT=QmT[base:base + 64, pr, tt * 128:(tt + 1) * 128],
                                                                               rhs=KMT[base:base + 64, pr, :], start=True, stop=True),
                       R=[("QmT", tt), "KMT"], W=[pg.name], rg=(base, 64))
                Dv(lambda e, pg=pg: e.tensor_tensor(out=gmv[:, 0:8, :], in0=pg[:, 0:128].rearrange("p (h k) -> p h k", h=8),
                                                    in1=MV[:, 0, b, :].unsqueeze(1).to_broadcast([128, 8, 16]), op=ALU.add), R=[pg.name, "MV"], W=["gmv"])
                for h in range(8):
                    Dv(lambda e, h=h: e.max(out=mx8[:, h, :], in_=gmv[:, h, :]), R=["gmv"], W=["mx8"])
                Dv(lambda e: e.tensor_tensor(out=gmk[:, :, :], in0=gmv[:, :, :], in1=mx8[:, :, 2:3].to_broadcast([128, 8, 16]), op=ALU.is_ge),
                   R=["gmv", "mx8"], W=["gmk"])
                Dv(lambda e: e.tensor_tensor(out=gmk[:, :, :], in0=gmk[:, :, :], in1=MV[:, 1, b, :].unsqueeze(1).to_broadcast([128, 8, 16]), op=ALU.mult),
                   R=["gmk", "MV"], W=["gmk"])
                Dv(lambda e: e.tensor_tensor(out=gmk[:, :, :], in0=gmk[:, :, :], in1=MV[:, 2, b, :].unsqueeze(1).to_broadcast([128, 8, 16]), op=ALU.add),
                   R=["gmk", "MV"], W=["gmk"])
                Dv(lambda e: e.tensor_scalar(out=mnegp[:, :, 0:3:2, 0:16], in0=gmk[:, :, :].rearrange("p (a c) k -> p a c k", a=4), scalar1=-NEG, scalar2=NEG,
                                             op0=ALU.mult, op1=ALU.add), R=["gmk"], W=["mnegp"])
                for sl_ in range(4):
                    Pe(lambda e, sl_=sl_: e.transpose(out=psT[0:96, sl_ * 128:(sl_ + 1) * 128], in_=mnegp[:, sl_, :, :].rearrange("p a b -> p (a b)"),
                                                      identity=ident_b[:]), R=["mnegp", "ident_b"], W=["psT"])
                Ac(lambda e, tt=tt: e.activation(out=MmT[:, :, tt * 128:(tt + 1) * 128], in_=psT[0:96, 0:512].rearrange("p (a b) -> p a b", a=4), func=AF.Copy),
                   R=["psT"], W=[("MmT", tt)])
            Dv(lambda e: e.tensor_reduce(out=kms[:], in_=KmB[:, :, :], axis=AX.X, op=ALU.add), R=["KmB"], W=["kms"])
            Dv(lambda e: e.tensor_scalar(out=KMT[:, :, b], in0=kms[:], scalar1=1.0 / 256, scalar2=None, op0=ALU.mult), R=["kms", "KMT"], W=["KMT"])

            if stop < 3:
                raise _Stop()
            pcm = gps()
            for g in range(2):
                for kv, (W1, RB, rk) in enumerate(((W1K, KcR, "KcR"), (W1V, VcR, "VcR"))):
                    col = (g * 2 + kv) * 16
                    for l in range(32):
                        Pe(lambda e, g=g, W1=W1, RB=RB, l=l, col=col: e.matmul(pcm[0:64, col:col + 16], lhsT=W1[g * 64:(g + 1) * 64, l, :],
                                                                            rhs=RB[g * 64:(g + 1) * 64, l:l + 241:16], start=(l == 0), stop=(l == 31)),
                           R=["W1K", "W1V", rk], W=[pcm.name], rg=(g * 64, 64))
            pcv = pcm[0:64, 0:64].rearrange("p (a k c) -> p a k c", a=2, k=2)
            for kv in range(2):
                Dv(lambda e, kv=kv: e.tensor_scalar(out=cx[:, :, :].rearrange("p (a k) c -> p a k c", a=2)[:, :, kv, :], in0=pcv[:, :, kv, :],
                                                    scalar1=CB[:, kv:kv + 1], scalar2=None, op0=ALU.add), R=[pcm.name, "CB"], W=["cx"])
            Dv(lambda e: e.tensor_tensor(out=cx2[:], in0=cx[:], in1=cx[:], op=ALU.mult), R=["cx"], W=["cx2"])
            Dv(lambda e: e.tensor_scalar(out=cx2[:], in0=cx2[:], scalar1=0.044715, scalar2=1.0, op0=ALU.mult, op1=ALU.add), R=["cx2"], W=["cx2"])
            Dv(lambda e: e.tensor_tensor(out=cx2[:], in0=cx2[:], in1=cx[:], op=ALU.mult), R=["cx2", "cx"], W=["cx2"])
            Ac(lambda e: e.activation(out=csg[:], in_=cx2[:], func=AF.Sigmoid, scale=1.5957691216057308), R=["cx2"], W=["csg"])
            Dv(lambda e: e.tensor_tensor(out=cg[:], in0=cx[:], in1=csg[:], op=ALU.mult), R=["cx", "csg"], W=["cg"])
            pk2 = gps()
            for g in range(2):
                Pe(lambda e, g=g: e.matmul(pk2[:, 0:16], lhsT=W2KP[:, g, :], rhs=cg[:, g * 2 + 0, :], start=(g == 0), stop=(g == 1)),
                   R=["W2KP", "cg"], W=[pk2.name], rg=(0, 64))
            Dv(lambda e: e.tensor_copy(out=KCT[:, 16 * b:16 * b + 16], in_=pk2[:, 0:16]), R=[pk2.name], W=["KCT"])
            pv2 = gps()
            for g in range(2):
                Pe(lambda e, g=g: e.matmul(pv2[0:16, g * 64:(g + 1) * 64], lhsT=cg[:, g * 2 + 1, :], rhs=W2V[:, :], start=True, stop=True),
                   R=["W2V", "cg"], W=[pv2.name], rg=(0, 64))
            Dv(lambda e: e.tensor_copy(out=vcs[:, :, 0:64], in_=pv2[0:16, 0:128].rearrange("p (g d) -> p g d", g=2)), R=[pv2.name], W=["vcs"])
            pbase = (16 * b) % 128
            s.dma("sp", VC[pbase:pbase + 16, (16 * b) // 128, :, :], vcs[:, :, :], R=["vcs"], W=["VC"])
            if b == 0:
                Dv(lambda e: e.memset(VC[0:1, 0, :, :].rearrange("p a b -> p (a b)"), 0.0), R=["VC"], W=["VC"])
            Dv(lambda e: e.tensor_copy(out=KcR[:, 0:16], in_=KcR[:, BT:BT + 16]), R=["KcR"], W=["KcR"])
            Dv(lambda e: e.tensor_copy(out=VcR[:, 0:16], in_=VcR[:, BT:BT + 16]), R=["VcR"], W=["VcR"])

            if stop < 4:
                raise _Stop()
            def finalize_branch(g, tt, bi, first):
                Ac(lambda e: e.activation(out=OTs[0:65, :], in_=psO[0:65, :], func=AF.Copy), R=["pso"], W=["OTs"])
                for r in range(4):
                    Pe(lambda e, r=r: e.transpose(out=psO2[:, r * 65:(r + 1) * 65], in_=OTs[0:65, r * 128:(r + 1) * 128], identity=ident_f[0:65, 0:65]),
                       R=["OTs", "ident_f"], W=["pso2"], rg=(0, 65))
                pv = psO2[:, 0:260].rearrange("p (r c) -> p r c", r=4)
                Dv(lambda e: e.tensor_scalar(out=rs4[:], in0=pv[:, :, 64], scalar1=1e-30, scalar2=None, op0=ALU.max), R=["pso2"], W=["rs4"])
                Dv(lambda e: e.reciprocal(out=rc4[:], in_=rs4[:]), R=["rs4"], W=["rc4"])
                gv = gates[:, tt, :].rearrange("p (h c) -> p h c", c=3)[:, 4 * g:4 * g + 4, bi]
                Dv(lambda e: e.tensor_tensor(out=cf4[:], in0=rc4[:], in1=gv, op=ALU.mult), R=["rc4", ("gates", tt)], W=["cf4"])
                if branch_only is not None and bi != branch_only:
                    if first:
                        Dv(lambda e: e.memset(yacc[:, :, :].rearrange("p r d -> p (r d)"), 0.0), W=["yacc"])
                    return
                if first:
                    Dv(lambda e: e.tensor_tensor(out=yacc[:], in0=pv[:, :, 0:64], in1=cf4[:].unsqueeze(2).to_broadcast([128, 4, 64]), op=ALU.mult),
                       R=["pso2", "cf4"], W=["yacc"])
                else:
                    Dv(lambda e: e.tensor_tensor(out=ytmp[:], in0=pv[:, :, 0:64], in1=cf4[:].unsqueeze(2).to_broadcast([128, 4, 64]), op=ALU.mult),
                       R=["pso2", "cf4"], W=["ytmp"])
                    Dv(lambda e: e.tensor_tensor(out=yacc[:], in0=yacc[:], in1=ytmp[:], op=ALU.add), R=["yacc", "ytmp"], W=["yacc"])

            pti = [0]

            def next_pT():
                i = pti[0] % 3
                pti[0] += 1
                return pT[i], f"pT{i}"

            for g in range(2):
                gp = slice(g * 64, (g + 1) * 64)
                for tt in range(2):
                    qt = 2 * b + tt
                    nkt = 1 if qt < 16 else 2
                    for kt in range(nkt):
                        ps = gps()
                        c0 = 128 * qt - 2048 * kt
                        need_mask = c0 < 2048
                        Pe(lambda e, kt=kt, ps=ps: e.matmul(ps[:, :].rearrange("p (r q) -> p r q", r=4), lhsT=KCT[gp, kt * 128:(kt + 1) * 128],
                                                            rhs=QnT[gp, :, tt * 128:(tt + 1) * 128], start=True, stop=not need_mask),
                           R=["KCT", ("QnT", tt)], W=[ps.name], rg=(g * 64, 64))
                        if need_mask:
                            Pe(lambda e, ps=ps, c0=c0: e.matmul(ps[:, :].rearrange("p (r q) -> p r q", r=4), lhsT=ident_b[:],
                                                              rhs=TC[:, c0:c0 + 128].unsqueeze(1).to_broadcast([128, 4, 128]), start=False, stop=True),
                               R=["ident_b", "TC"], W=[ps.name])
                        Ac(lambda e, kt=kt, ps=ps: e.activation(out=pc[kt][:], in_=ps[:], func=AF.Exp, scale=0.125), R=[ps.name], W=[f"pc{kt}"])
                    for kt in range(nkt):
                        Pe(lambda e, kt=kt: e.matmul(psO[0:65, :], lhsT=VC[:, kt, g, :], rhs=pc[kt][:], start=(kt == 0), stop=(kt == nkt - 1)),
                           R=["VC", f"pc{kt}"], W=["pso"])
                    for r in range(4):
                        for kt in range(nkt):
                            Pe(lambda e, kt=kt, r=r: e.matmul(psO2[:, r * 65:(r + 1) * 65], lhsT=pc[kt][:, r * 128:(r + 1) * 128], rhs=OVL[:, kt, :],
                                                              start=(kt == 0), stop=(kt == nkt - 1)), R=["OVL", f"pc{kt}"], W=["pso2"])
                    pv = psO2[:, 0:260].rearrange("p (r c) -> p r c", r=4)
                    Dv(lambda e: e.tensor_scalar(out=rs4[:], in0=pv[:, :, 64], scalar1=1e-30, scalar2=None, op0=ALU.max), R=["pso2"], W=["rs4"])
                    Dv(lambda e: e.reciprocal(out=rc4[:], in_=rs4[:]), R=["rs4"], W=["rc4"])
                    Dv(lambda e: e.tensor_scalar(out=imp[:], in0=pv[:, 0, 0:64], scalar1=rc4[:, 0:1], scalar2=None, op0=ALU.mult), R=["pso2", "rc4"], W=["imp"])
                    for r in range(1, 4):
                        Dv(lambda e, r=r: e.scalar_tensor_tensor(out=imp[:], in0=pv[:, r, 0:64], scalar=rc4[:, r:r + 1], in1=imp[:], op0=ALU.mult, op1=ALU.add),
                           R=["pso2", "rc4", "imp"], W=["imp"])
                    Dv(lambda e: e.tensor_tensor(out=imp[:], in0=imp[:], in1=F0[:, 63 - 2 * qt:127 - 2 * qt], op=ALU.add), R=["imp", "F0"], W=["imp"])
                    Dv(lambda e: e.memset(imp[:, 0:1], BIG), R=["imp"], W=["imp"])
                    Dv(lambda e: e.max(out=m8a[:], in_=imp[:]), R=["imp"], W=["m8a"])
                    Dv(lambda e: e.match_replace(out=imp2[:], in_to_replace=m8a[:], in_values=imp[:], imm_value=-2.0 * BIG), R=["imp", "m8a"], W=["imp2"])
                    Dv(lambda e: e.max(out=m8b[:], in_=imp2[:]), R=["imp2"], W=["m8b"])
                    Dv(lambda e: e.tensor_scalar(out=imp2[:], in0=imp[:], scalar1=m8b[:, 7:8], scalar2=None, op0=ALU.is_ge), R=["imp", "m8b"], W=["imp2"])
                    Dv(lambda e: e.tensor_scalar(out=mselb[:], in0=imp2[:], scalar1=-NEG, scalar2=NEG, op0=ALU.mult, op1=ALU.add), R=["imp2"], W=["mselb"])
                    Pe(lambda e: e.transpose(out=psT[0:64, 0:128], in_=mselb[:, :], identity=ident_b[:]), R=["mselb", "ident_b"], W=["psT"])
                    Dv(lambda e: e.tensor_copy(out=MselT[gp, :], in_=psT[0:64, 0:128]), R=["psT"], W=["MselT"])
                    finalize_branch(g, tt, 0, True)
                    qrhs = QrT[gp, :, tt * 128:(tt + 1) * 128]
                    for kt in range(qt + 1):
                        ps = gps()
                        psv = ps[:, :].rearrange("p (r q) -> p r q", r=4)
                        Pe(lambda e, kt=kt, psv=psv: e.matmul(psv, lhsT=KsT[gp, kt * 128:(kt + 1) * 128], rhs=qrhs, start=True, stop=False),
                           R=[("KsT", kt), ("QrT", tt)], W=[ps.name], rg=(g * 64, 64))
                        Pe(lambda e, kt=kt, psv=psv: e.matmul(psv, lhsT=ESEL[gp, kt * 128:(kt + 1) * 128], rhs=MselT[gp, :].unsqueeze(1).to_broadcast([64, 4, 128]),
                                                              start=False, stop=(kt != qt)), R=["ESEL", "MselT"], W=[ps.name], rg=(g * 64, 64))
                        if kt == qt:
                            Pe(lambda e, psv=psv: e.matmul(psv, lhsT=ident_b[:], rhs=CAUS[:, :].unsqueeze(1).to_broadcast([128, 4, 128]), start=False, stop=True),
                               R=["ident_b", "CAUS"], W=[ps.name])
                        pt, ptk = next_pT()
                        Ac(lambda e, ps=ps, pt=pt: e.activation(out=pt[:], in_=ps[:], func=AF.Exp, scale=0.125), R=[ps.name], W=[ptk])
                        Pe(lambda e, kt=kt, pt=pt: e.matmul(psO[0:65, :], lhsT=VsA[:, kt, g, :], rhs=pt[:], start=(kt == 0), stop=(kt == qt)),
                           R=[("VsA", kt), ptk], W=["pso"])
                    finalize_branch(g, tt, 1, False)
                    kts = list(range(max(0, qt - 4), qt + 1))
                    for kt in kts:
                        ps = gps()
                        psv = ps[:, :].rearrange("p (r q) -> p r q", r=4)
                        msk = CAUS if kt == qt else (CAUS2 if kt == qt - 4 else None)
                        Pe(lambda e, kt=kt, psv=psv, msk=msk: e.matmul(psv, lhsT=KwT[gp, kt * 128:(kt + 1) * 128], rhs=qrhs, start=True, stop=(msk is None)),
                           R=[("KwT", kt), ("QrT", tt)], W=[ps.name], rg=(g * 64, 64))
                        if msk is not None:
                            Pe(lambda e, psv=psv, msk=msk: e.matmul(psv, lhsT=ident_b[:], rhs=msk[:, :].unsqueeze(1).to_broadcast([128, 4, 128]), start=False, stop=True),
                               R=["ident_b", "CAUS", "CAUS2"], W=[ps.name])
                        pt, ptk = next_pT()
                        Ac(lambda e, ps=ps, pt=pt: e.activation(out=pt[:], in_=ps[:], func=AF.Exp, scale=0.125), R=[ps.name], W=[ptk])
                        Pe(lambda e, kt=kt, pt=pt: e.matmul(psO[0:65, :], lhsT=VwA[:, kt, g, :], rhs=pt[:], start=(kt == kts[0]), stop=(kt == kts[-1])),
                           R=[("VwA", kt), ptk], W=["pso"])
                    finalize_branch(g, tt, 2, False)
                    Dv(lambda e, tt=tt, g=g: e.tensor_copy(out=yn_tm[:, tt, g * 256:(g + 1) * 256], in_=yacc[:, :, :].rearrange("p r d -> p (r d)")),
                       R=["yacc"], W=[("yn_tm", tt)])

            if stop < 5:
                raise _Stop()
            nkt_m = 2 * b + 2
            for hg in range(4):
                for kt in range(nkt_m):
                    ri = (hg * nkt_m + kt) % 3
                    kr, vr = kring[ri], vring[ri]
                    s.dma("sp", kr[:, :, :], kmd[kt].rearrange("p (a b) -> p a b", a=4), R=[("kmd", kt)], W=[f"kring{ri}"])
                    s.dma("sp", vr[:, :, :], vmd[kt].rearrange("p (a b) -> p a b", a=8), R=[("vmd", kt)], W=[f"vring{ri}"])
                    ktl = kt - 2 * b
                    for hh in range(2):
                        h = hg * 2 + hh
                        pr, base = h // 2, 64 * (h % 2)
                        slot, sb = h // 2, 64 * (h % 2)
                        q0 = 0 if ktl < 0 else 128 * ktl
                        ps = gps()
                        Pe(lambda e, ps=ps, kr=kr, pr=pr, base=base, q0=q0: e.matmul(ps[:, q0:BT], lhsT=kr[base:base + 64, pr, :], rhs=QmT[base:base + 64, pr, q0:BT],
                                                                                   start=True, stop=False), R=[f"kring{ri}", "QmT"], W=[ps.name], rg=(base, 64))
                        if ktl < 0:
                            Pe(lambda e, ps=ps, slot=slot, sb=sb, kt=kt: e.matmul(ps[:, 0:BT], lhsT=EM[sb:sb + 16, kt // 2, :], rhs=MmT[sb:sb + 16, slot, :],
                                                                                  start=False, stop=True), R=["EM", "MmT"], W=[ps.name], rg=(sb, 16))
                        else:
                            Pe(lambda e, ps=ps, q0=q0: e.matmul(ps[:, q0:BT], lhsT=ident_b[:], rhs=CAUSW[:, 0:BT - q0], start=False, stop=True),
                               R=["ident_b", "CAUSW"], W=[ps.name])
                        pt, ptk = next_pT()
                        Ac(lambda e, ps=ps, pt=pt, q0=q0: e.activation(out=pt[:, q0:BT], in_=ps[:, q0:BT], func=AF.Exp, scale=0.125), R=[ps.name], W=[ptk])
                        pm = psM[hh]
                        Pe(lambda e, pm=pm, vr=vr, h=h, pt=pt, q0=q0, kt=kt: e.matmul(pm[0:65, q0:BT], lhsT=vr[:, h, :], rhs=pt[:, q0:BT],
                                                                                     start=(kt == 0), stop=(kt == nkt_m - 1)),
                           R=[f"vring{ri}", ptk], W=[pm.name])
                for hh in range(2):
                    h = hg * 2 + hh
                    pm = psM[hh]
                    Ac(lambda e, pm=pm: e.activation(out=OTs[0:65, 0:BT], in_=pm[0:65, 0:BT], func=AF.Copy), R=[pm.name], W=["OTs"])
                    for tt in range(2):
                        Pe(lambda e, tt=tt: e.transpose(out=psO2[:, tt * 65:(tt + 1) * 65], in_=OTs[0:65, tt * 128:(tt + 1) * 128], identity=ident_f[0:65, 0:65]),
                           R=["OTs", "ident_f"], W=["pso2"], rg=(0, 65))
                    pv = psO2[:, 0:130].rearrange("p (r c) -> p r c", r=2)
                    Dv(lambda e: e.tensor_scalar(out=rs4[:, 0:2], in0=pv[:, :, 64], scalar1=1e-30, scalar2=None, op0=ALU.max), R=["pso2"], W=["rs4"])
                    Dv(lambda e: e.reciprocal(out=rc4[:, 0:2], in_=rs4[:, 0:2]), R=["rs4"], W=["rc4"])
                    Dv(lambda e, h=h: e.tensor_tensor(out=ym_tm[:, :, h * 64:(h + 1) * 64], in0=pv[:, :, 0:64],
                                                      in1=rc4[:, 0:2].unsqueeze(2).to_broadcast([128, 2, 64]), op=ALU.mult),
                       R=["pso2", "rc4"], W=["ym_tm"])

            if debug:
                for tt in range(2):
                    r0 = tok0 + tt * 128
                    Dv(lambda e, tt=tt: e.tensor_copy(out=x1[:, 0, 0:512], in_=yn_tm[:, tt, :]), R=[("yn_tm", tt)], W=[("x1", 0)])
                    s.dma("sp", dbg_yn[r0:r0 + 128, :], x1[:, 0, 0:512], R=[("x1", 0)], W=["dbg"])
                    Dv(lambda e, tt=tt: e.tensor_copy(out=x1[:, 1, 0:512], in_=ym_tm[:, tt, :]), R=["ym_tm"], W=[("x1", 1)])
                    s.dma("sp", dbg_ym[r0:r0 + 128, :], x1[:, 1, 0:512], R=[("x1", 1)], W=["dbg"])

            if stop < 6:
                raise _Stop()
            pool7 = allps
            for tt in range(2):
                for c in range(4):
                    Pe(lambda e, c=c, tt=tt: e.transpose(out=psT[:, c * 128:(c + 1) * 128], in_=yn_tm[:, tt, c * 128:(c + 1) * 128], identity=ident_b[:]),
                       R=[("yn_tm", tt), "ident_b"], W=["psT"])
                    Pe(lambda e, c=c, tt=tt: e.transpose(out=psT[:, 512 + c * 128:512 + (c + 1) * 128], in_=ym_tm[:, tt, c * 128:(c + 1) * 128], identity=ident_b[:]),
                       R=["ym_tm", "ident_b"], W=["psT"])
                Ac(lambda e, tt=tt: e.activation(out=ynT[:, :, tt * 128:(tt + 1) * 128], in_=psT[:, 0:512].rearrange("p (a b) -> p a b", a=4), func=AF.Copy),
                   R=["psT"], W=["ynT"])
                Dv(lambda e, tt=tt: e.tensor_copy(out=ymT[:, :, tt * 128:(tt + 1) * 128], in_=psT[:, 512:1024].rearrange("p (a b) -> p a b", a=4)),
                   R=["psT"], W=["ymT"])
            kun, kum = "wun_t", "wum_t"
            s.dma("pool", wun_t[:, :, :], w_un.rearrange("(c p) n -> p c n", p=128), W=[kun])
            s.dma("pool", wum_t[:, :, :], w_um.rearrange("(c p) n -> p c n", p=128), W=[kum])
            wun = wun_t
            wum = wum_t
            for jq in range(4):
                sgl, kgl = next_slab()
                glv = sgl[:, :].rearrange("p (a b) -> p a b", a=8)
                s.dma("pool", glv[:, :, 0:256], w_in[:, OFF_GM + jq * 256:OFF_GM + (jq + 1) * 256].rearrange("(kc p) n -> p kc n", p=128), W=[kgl])
                s.dma("pool", glv[:, :, 256:512], w_in[:, OFF_GM + 1024 + jq * 256:OFF_GM + 1024 + (jq + 1) * 256].rearrange("(kc p) n -> p kc n", p=128),
                      R=[kgl], W=[kgl])
                for j2 in range(2):
                    jc = jq * 2 + j2
                    pA, pB, pC, pD = gps(pool7), gps(pool7), gps(pool7), gps(pool7)
                    for c in range(4):
                        Pe(lambda e, c=c, pA=pA, jc=jc: e.matmul(pA[:, 0:BT], lhsT=wun[:, c, jc * 128:(jc + 1) * 128], rhs=ynT[:, c, :], start=(c == 0), stop=(c == 3)),
                           R=[kun, "ynT"], W=[pA.name])
                    for c in range(4):
                        Pe(lambda e, c=c, pB=pB, jc=jc: e.matmul(pB[:, 0:BT], lhsT=wum[:, c, jc * 128:(jc + 1) * 128], rhs=ymT[:, c, :], start=(c == 0), stop=(c == 3)),
                           R=[kum, "ymT"], W=[pB.name])
                    for kc in range(8):
                        Pe(lambda e, kc=kc, pC=pC, j2=j2: e.matmul(pC[:, 0:BT], lhsT=glv[:, kc, j2 * 128:(j2 + 1) * 128], rhs=hT[:, kc, :], start=(kc == 0), stop=(kc == 7)),
                           R=[kgl, "hT"], W=[pC.name])
                    for kc in range(8):
                        Pe(lambda e, kc=kc, pD=pD, j2=j2: e.matmul(pD[:, 0:BT], lhsT=glv[:, kc, 256 + j2 * 128:256 + (j2 + 1) * 128], rhs=hT[:, kc, :],
                                                                   start=(kc == 0), stop=(kc == 7)), R=[kgl, "hT"], W=[pD.name])
                    Ac(lambda e, pC=pC: e.activation(out=sg[0][:], in_=pC[:, 0:BT], func=AF.Sigmoid), R=[pC.name], W=["sg0"])
                    Ac(lambda e, pD=pD: e.activation(out=sg[1][:], in_=pD[:, 0:BT], func=AF.Sigmoid), R=[pD.name], W=["sg1"])
                    Dv(lambda e, pA=pA: e.tensor_tensor(out=sg[2][:], in0=pA[:, 0:BT], in1=sg[0][:], op=ALU.mult), R=[pA.name, "sg0"], W=["sg2"])
                    Dv(lambda e, pB=pB: e.tensor_tensor(out=sg[3][:], in0=pB[:, 0:BT], in1=sg[1][:], op=ALU.mult), R=[pB.name, "sg1"], W=["sg3"])
                    Dv(lambda e, jc=jc: e.tensor_tensor(out=mixT[:, jc, :], in0=sg[2][:], in1=sg[3][:], op=ALU.add), R=["sg2", "sg3"], W=[("mixT", jc)])

            for half in range(2):
                so, ko = load_w_slab(w_out[:, half * 512:(half + 1) * 512].rearrange("(kc p) n -> p kc n", p=128), 512)
                sov = so[:, :].rearrange("p (a b) -> p a b", a=8)
                for tt in range(2):
                    ps = gps(pool7)
                    for jc in range(8):
                        Pe(lambda e, jc=jc, tt=tt, ps=ps: e.matmul(ps[:, :], lhsT=mixT[:, jc, tt * 128:(tt + 1) * 128], rhs=sov[:, jc, :], start=(jc == 0), stop=(jc == 7)),
                           R=["mixT", ko], W=[ps.name])
                    Dv(lambda e, tt=tt, half=half, ps=ps: e.tensor_tensor(out=x1[:, tt, half * 512:(half + 1) * 512], in0=ps[:, :],
                                                                          in1=xbuf[:, tt, half * 512:(half + 1) * 512], op=ALU.add),
                       R=[ps.name, ("xbuf", tt)], W=[("x1", tt)])
            if debug:
                for tt in range(2):
                    r0 = tok0 + tt * 128
                    s.dma("sp", dbg_x1[r0:r0 + 128, :], x1[:, tt, :], R=[("x1", tt)], W=["dbg"])
            if stop < 7:
                raise _Stop()
            for tt in range(2):
                rmsnorm_tile(x1[:, tt, :], [("x1", tt)], G2, "G2", 2 + tt, h_tm[:], ["h_tm"])
                to_hT(tt)
            for fh in range(2):
                for sidx in range(4):
                    c0 = fh * 2048 + sidx * 512
                    sw, kw = load_w_slab(w_ff1[:, c0:c0 + 512].rearrange("(kc p) n -> p kc n", p=128), 512)
                    swv = sw[:, :].rearrange("p (a b) -> p a b", a=8)
                    for fcl in range(4):
                        fci = sidx * 4 + fcl
                        ps = gps(pool7)
                        for kc in range(8):
                            Pe(lambda e, kc=kc, ps=ps, fcl=fcl: e.matmul(ps[:, 0:BT], lhsT=swv[:, kc, fcl * 128:(fcl + 1) * 128], rhs=hT[:, kc, :],
                                                                       start=(kc == 0), stop=(kc == 7)), R=[kw, "hT"], W=[ps.name])
                        rt = rl[fci % 2]
                        Ac(lambda e, ps=ps, rt=rt: e.activation(out=rt[:], in_=ps[:, 0:BT], func=AF.Relu), R=[ps.name], W=[f"rl{fci % 2}"])
                        Dv(lambda e, rt=rt, fci=fci: e.tensor_tensor(out=uT[:, fci, :], in0=rt[:], in1=rt[:], op=ALU.mult), R=[f"rl{fci % 2}"], W=[("uT", fci)])
                for jc in range(8):
                    sw, kw = next_slab()
                    swv = sw[:, 0:2048].rearrange("p (a b) -> p a b", a=16)
                    s.dma("pool", swv, w_ff2[fh * 2048:(fh + 1) * 2048, jc * 128:(jc + 1) * 128].rearrange("(fc p) n -> p fc n", p=128), W=[kw])
                    ps = gps(pool7)
                    for fci in range(16):
                        Pe(lambda e, fci=fci, ps=ps: e.matmul(ps[:, 0:BT], lhsT=swv[:, fci, :], rhs=uT[:, fci, :], start=(fci == 0), stop=(fci == 15)),
                           R=[kw, "uT"], W=[ps.name])
                    st = stg[jc % 2]
                    Ac(lambda e, ps=ps, st=st: e.activation(out=st[:], in_=ps[:, 0:BT], func=AF.Copy), R=[ps.name], W=[f"stg{jc % 2}"])
                    px = gps(pool7)
                    for tt in range(2):
                        Pe(lambda e, tt=tt, px=px, st=st: e.transpose(out=px[:, tt * 128:(tt + 1) * 128], in_=st[:, tt * 128:(tt + 1) * 128], identity=ident_f[:]),
                           R=[f"stg{jc % 2}", "ident_f"], W=[px.name])
                    Dv(lambda e, px=px, jc=jc: e.tensor_tensor(out=x1[:, :, jc * 128:(jc + 1) * 128], in0=px[:, 0:256].rearrange("p (t j) -> p t j", t=2),
                                                               in1=x1[:, :, jc * 128:(jc + 1) * 128], op=ALU.add), R=[px.name, "x1"], W=["x1"])
            for tt in range(2):
                r0 = tok0 + tt * 128
                rmsnorm_tile(x1[:, tt, :], [("x1", tt)], GF, "GF", tt, xbuf[:, tt, :], [("xbuf", tt)])
                s.dma("sp", y[r0:r0 + 128, :], xbuf[:, tt, :], R=[("xbuf", tt)], W=["y"])
        for b in range(nb_run):
            try:
                run_block(b)
            except _Stop:
                pass
        keys = ["y"] + (["dbg"] if debug else [])
        s.finish(keys, "sp")
        print("instructions:", s.ninst, {k: v for k, v in s.cnt.items()})
    return nc


_CONSTS = None


def kernel(**inputs):
    global _CONSTS
    if _CONSTS is None:
        _CONSTS = make_consts()
    n = 8
    nc = build_nc()
    xs = np.asarray(inputs["x"], dtype=np.float32)
    shared = {}
    for k in ("w_in", "cmp_pe_k", "cmp_pe_v", "cmp_k_w1", "cmp_k_w2", "cmp_v_w1", "cmp_v_w2", "w_up_nsa", "w_up_moba",
              "w_out", "w_ff1", "w_ff2"):
        shared[k] = np.ascontiguousarray(np.asarray(inputs[k], dtype=np.float32)[0])
    shared["norm1_g"] = np.ascontiguousarray(np.asarray(inputs["norm1_g"], dtype=np.float32).reshape(1, D))
    shared["norm2_g"] = np.ascontiguousarray(np.asarray(inputs["norm2_g"], dtype=np.float32).reshape(1, D))
    shared["norm_f_g"] = np.ascontiguousarray(np.asarray(inputs["norm_f_g"], dtype=np.float32).reshape(1, D))
    shared.update(_CONSTS)
    in_maps = []
    for i in range(n):
        m = dict(shared)
        m["x"] = np.ascontiguousarray(xs[i])
        in_maps.append(m)
    res = run_bass_kernel_spmd(nc, in_maps, core_ids=list(range(n)))
    out = np.stack([np.asarray(res.results[i]["y"], dtype=np.float32) for i in range(n)], axis=0)
    return out
```
